# Optimizing a Trainium2 kernel written in Bass

```python
import jax, jax.numpy as jnp
from jax import lax
import numpy as np

D_MODEL = 2048
BATCH = 4
SEQ = 2048
DEPTH = 1

FOX_HEADS = 8
FOX_HEAD_DIM = 128
FOX_WIDTH = FOX_HEADS * FOX_HEAD_DIM
NSA_HEADS = 8
NSA_KV_GROUPS = 2
NSA_HPG = NSA_HEADS // NSA_KV_GROUPS
NSA_QK_DIM = 192
NSA_V_DIM = 128
NSA_WIDTH = NSA_HEADS * NSA_V_DIM
CMP_BLOCK = 32
CMP_STRIDE = 16
CMP_HIDDEN = 256
SEL_BLOCK = 64
SEL_TOPK = 16
SEL_LOCAL = 2
FORCE_SCORE = 1.0e4
WINDOW = 512
Q_BLOCK = 128
SEL_Q_CHUNK = 32
KV_K = NSA_KV_GROUPS * NSA_QK_DIM
KV_V = NSA_KV_GROUPS * NSA_V_DIM
D_FF = -(-(8 * D_MODEL) // (3 * 256)) * 256
RMS_EPS = 1e-6
IN_SPLITS = (FOX_WIDTH, FOX_WIDTH, FOX_WIDTH, FOX_HEADS,
             NSA_HEADS * NSA_QK_DIM, KV_K, KV_V, KV_K, KV_V, KV_K, KV_V,
             3 * NSA_HEADS, D_MODEL, D_MODEL)
D_IN = sum(IN_SPLITS)

kernel_name = "hybrid_fox_nsa_gated_parallel_block"


def _rms(x, gain):
    xf = x.astype(jnp.float32)
    y = xf * lax.rsqrt(jnp.mean(xf * xf, axis=-1, keepdims=True) + RMS_EPS)
    return (y * gain.astype(jnp.float32)).astype(x.dtype)


def _masked_softmax(s, mask):
    s = jnp.where(mask, s, -jnp.inf)
    m = jnp.max(s, axis=-1, keepdims=True)
    m = jnp.where(jnp.isfinite(m), m, 0.0)
    e = jnp.where(mask, jnp.exp(s - m), 0.0)
    return e / jnp.maximum(jnp.sum(e, axis=-1, keepdims=True), 1e-30)


def _alibi_slopes(n):
    return jnp.exp2(-8.0 * jnp.arange(1, n + 1, dtype=jnp.float32) / n)


def _fox_attention(q, k, v, log_f):
    B, T, H, Dh = q.shape
    nb = T // Q_BLOCK
    c = jnp.cumsum(log_f, axis=1).transpose(0, 2, 1)
    kh = k.transpose(0, 2, 1, 3)
    vh = v.transpose(0, 2, 1, 3)
    qb = q.reshape(B, nb, Q_BLOCK, H, Dh).transpose(1, 0, 3, 2, 4)
    cb = c.reshape(B, H, nb, Q_BLOCK).transpose(2, 0, 1, 3)
    scale = Dh ** -0.5
    kpos = jnp.arange(T)

    def block(args):
        qi, ci, i = args
        qpos = i * Q_BLOCK + jnp.arange(Q_BLOCK)
        s = jnp.einsum('bhqd,bhkd->bhqk', qi, kh, preferred_element_type=jnp.float32) * scale
        s = s + ci[..., :, None] - c[:, :, None, :]
        p = _masked_softmax(s, kpos[None, :] <= qpos[:, None])
        return jnp.einsum('bhqk,bhkd->bhqd', p.astype(vh.dtype), vh)

    o = lax.map(block, (qb, cb, jnp.arange(nb)))
    return o.transpose(1, 0, 3, 2, 4).reshape(B, T, H * Dh)


def _compress(z, pe, w1, w2):
    B, T, G, D = z.shape
    r = CMP_BLOCK // CMP_STRIDE
    ch = z.reshape(B, T // CMP_STRIDE, CMP_STRIDE, G, D)
    nc = T // CMP_STRIDE - r + 1
    blk = jnp.concatenate([ch[:, i:i + nc] for i in range(r)], axis=2)
    blk = blk + pe[None, None, :, None, :]
    flat = blk.transpose(0, 1, 3, 2, 4).reshape(B, nc, G, CMP_BLOCK * D)
    return jax.nn.silu(flat @ w1) @ w2


def _overlap_matrix(nc, ns):
    i = np.arange(nc)[:, None]
    j = np.arange(ns)[None, :]
    lo = np.maximum(i * CMP_STRIDE, j * SEL_BLOCK)
    hi = np.minimum(i * CMP_STRIDE + CMP_BLOCK, (j + 1) * SEL_BLOCK)
    return (np.maximum(hi - lo, 0) / CMP_STRIDE).astype(np.float32)


def _nsa_attention(q, kc_raw, vc_raw, ks_raw, vs_raw, kw_raw, vw_raw, gate_logits,
                   q_gain, kc_gain, ks_gain, kw_gain,
                   cmp_pe_k, cmp_w1_k, cmp_w2_k, cmp_pe_v, cmp_w1_v, cmp_w2_v):
    B, T, _ = q.shape
    G, HPG, Dk, Dv = NSA_KV_GROUPS, NSA_HPG, NSA_QK_DIM, NSA_V_DIM
    scale = Dk ** -0.5
    slopes = _alibi_slopes(NSA_HEADS).reshape(G, HPG)
    tpos = jnp.arange(T)
    qn = _rms(q.reshape(B, T, G, HPG, Dk), q_gain)
    qc = qn.transpose(0, 2, 3, 1, 4)

    kc = _rms(_compress(kc_raw.reshape(B, T, G, Dk), cmp_pe_k, cmp_w1_k, cmp_w2_k), kc_gain)
    vc = _compress(vc_raw.reshape(B, T, G, Dv), cmp_pe_v, cmp_w1_v, cmp_w2_v)
    nc = kc.shape[1]
    cend = jnp.arange(nc) * CMP_STRIDE + CMP_BLOCK - 1
    dist_c = tpos[:, None] - cend[None, :]
    s_c = jnp.einsum('btghd,bngd->bghtn', qn, kc, preferred_element_type=jnp.float32) * scale
    s_c = s_c - slopes[None, :, :, None, None] * dist_c.astype(jnp.float32)
    p_cmp = _masked_softmax(s_c, dist_c >= 0)
    o_cmp = jnp.einsum('bghtn,bngd->btghd', p_cmp.astype(vc.dtype), vc)

    ns = T // SEL_BLOCK
    n_sel = min(SEL_TOPK, ns)
    imp = jnp.einsum('bgtn,nj->bgtj', p_cmp.sum(axis=2), jnp.asarray(_overlap_matrix(nc, ns)))
    cur = tpos // SEL_BLOCK
    jidx = jnp.arange(ns)
    back = cur[:, None] - jidx[None, :]
    eligible = back >= 0
    forced = (jidx[None, :] == 0) | (eligible & (back < SEL_LOCAL))
    score = jnp.where(eligible, jnp.where(forced, FORCE_SCORE, imp), -1.0)
    top_score, idx = lax.top_k(score, n_sel)
    valid = top_score >= 0.0

    ks_b = _rms(ks_raw.reshape(B, T, G, Dk), ks_gain).transpose(0, 2, 1, 3).reshape(B, G, ns, SEL_BLOCK, Dk)
    vs_b = vs_raw.reshape(B, T, G, Dv).transpose(0, 2, 1, 3).reshape(B, G, ns, SEL_BLOCK, Dv)
    nq = T // SEL_Q_CHUNK
    q_ch = qc.reshape(B, G, HPG, nq, SEL_Q_CHUNK, Dk).transpose(3, 0, 1, 2, 4, 5)
    idx_ch = idx.reshape(B, G, nq, SEL_Q_CHUNK, n_sel).transpose(2, 0, 1, 3, 4)
    val_ch = valid.reshape(B, G, nq, SEL_Q_CHUNK, n_sel).transpose(2, 0, 1, 3, 4)
    bi = jnp.arange(B)[:, None, None, None]
    gi = jnp.arange(G)[None, :, None, None]
    in_blk = jnp.arange(SEL_BLOCK)

    def sel_chunk(args):
        qi, ii, vi, c = args
        kg = ks_b[bi, gi, ii]
        vg = vs_b[bi, gi, ii].reshape(B, G, SEL_Q_CHUNK, n_sel * SEL_BLOCK, Dv)
        qpos = c * SEL_Q_CHUNK + jnp.arange(SEL_Q_CHUNK)
        kpos = ii[..., None] * SEL_BLOCK + in_blk
        dist = qpos[None, None, :, None, None] - kpos
        mask = ((dist >= 0) & vi[..., None]).reshape(B, G, 1, SEL_Q_CHUNK, n_sel * SEL_BLOCK)
        s = jnp.einsum('bghqd,bgqskd->bghqsk', qi, kg, preferred_element_type=jnp.float32) * scale
        s = s - slopes[None, :, :, None, None, None] * dist[:, :, None].astype(jnp.float32)
        p = _masked_softmax(s.reshape(B, G, HPG, SEL_Q_CHUNK, n_sel * SEL_BLOCK), mask)
        return jnp.einsum('bghqk,bgqkd->bghqd', p.astype(vg.dtype), vg)

    o_slc = lax.map(sel_chunk, (q_ch, idx_ch, val_ch, jnp.arange(nq)))
    o_slc = o_slc.transpose(1, 0, 4, 2, 3, 5).reshape(B, T, G, HPG, Dv)

    nb = T // Q_BLOCK
    r = WINDOW // Q_BLOCK
    kw = _rms(kw_raw.reshape(B, T, G, Dk), kw_gain).transpose(0, 2, 1, 3)
    vw = vw_raw.reshape(B, T, G, Dv).transpose(0, 2, 1, 3)
    pad = ((0, 0), (0, 0), (WINDOW, 0), (0, 0))
    kwb = jnp.pad(kw, pad).reshape(B, G, nb + r, Q_BLOCK, Dk)
    vwb = jnp.pad(vw, pad).reshape(B, G, nb + r, Q_BLOCK, Dv)
    band_k = jnp.concatenate([kwb[:, :, i:i + nb] for i in range(r + 1)], axis=3)
    band_v = jnp.concatenate([vwb[:, :, i:i + nb] for i in range(r + 1)], axis=3)
    kb_len = (r + 1) * Q_BLOCK
    dist_w = jnp.arange(Q_BLOCK)[:, None] - jnp.arange(kb_len)[None, :] + WINDOW
    kpos_w = jnp.arange(nb)[:, None] * Q_BLOCK - WINDOW + jnp.arange(kb_len)[None, :]
    mask_w = (dist_w >= 0)[None] & (dist_w < WINDOW)[None] & (kpos_w >= 0)[:, None, :]
    q_w = qc.reshape(B, G, HPG, nb, Q_BLOCK, Dk)
    s_w = jnp.einsum('bghnqd,bgnkd->bghnqk', q_w, band_k, preferred_element_type=jnp.float32) * scale
    s_w = s_w - slopes[None, :, :, None, None, None] * dist_w.astype(jnp.float32)
    p_w = _masked_softmax(s_w, mask_w)
    o_win = jnp.einsum('bghnqk,bgnkd->bghnqd', p_w.astype(band_v.dtype), band_v)
    o_win = o_win.reshape(B, G, HPG, T, Dv).transpose(0, 3, 1, 2, 4)

    g = jax.nn.sigmoid(gate_logits.astype(jnp.float32)).reshape(B, T, G, HPG, 3)
    o = g[..., 0:1] * o_cmp + g[..., 1:2] * o_slc + g[..., 2:3] * o_win
    return o.reshape(B, T, NSA_WIDTH).astype(q.dtype)


def setup_inputs(seed: int = 0) -> dict:
    key = jax.random.key(seed)
    ks = jax.random.split(key, 24)
    f32 = jnp.float32
    L = DEPTH

    def nrm(k, shape, fan_in):
        return jax.random.normal(k, shape, f32) * fan_in ** -0.5

    def gain(k, shape):
        return 1.0 + 0.02 * jax.random.normal(k, shape, f32)

    return {
        "x": jax.random.normal(ks[0], (BATCH, SEQ, D_MODEL), f32),
        "norm_attn": gain(ks[1], (L, D_MODEL)),
        "w_in": nrm(ks[2], (L, D_MODEL, D_IN), D_MODEL),
        "fox_f_bias": 3.0 + 0.1 * jax.random.normal(ks[3], (L, FOX_HEADS), f32),
        "fox_q_gain": gain(ks[4], (L, FOX_HEAD_DIM)),
        "fox_k_gain": gain(ks[5], (L, FOX_HEAD_DIM)),
        "nsa_q_gain": gain(ks[6], (L, NSA_QK_DIM)),
        "nsa_kc_gain": gain(ks[7], (L, NSA_QK_DIM)),
        "nsa_ks_gain": gain(ks[8], (L, NSA_QK_DIM)),
        "nsa_kw_gain": gain(ks[9], (L, NSA_QK_DIM)),
        "cmp_pe_k": 0.02 * jax.random.normal(ks[10], (L, CMP_BLOCK, NSA_QK_DIM), f32),
        "cmp_w1_k": nrm(ks[11], (L, CMP_BLOCK * NSA_QK_DIM, CMP_HIDDEN), CMP_BLOCK * NSA_QK_DIM),
        "cmp_w2_k": nrm(ks[12], (L, CMP_HIDDEN, NSA_QK_DIM), CMP_HIDDEN),
        "cmp_pe_v": 0.02 * jax.random.normal(ks[13], (L, CMP_BLOCK, NSA_V_DIM), f32),
        "cmp_w1_v": nrm(ks[14], (L, CMP_BLOCK * NSA_V_DIM, CMP_HIDDEN), CMP_BLOCK * NSA_V_DIM),
        "cmp_w2_v": nrm(ks[15], (L, CMP_HIDDEN, NSA_V_DIM), CMP_HIDDEN),
        "w_up_fox": nrm(ks[16], (L, FOX_WIDTH, D_MODEL), FOX_WIDTH),
        "w_up_nsa": nrm(ks[17], (L, NSA_WIDTH, D_MODEL), NSA_WIDTH),
        "w_out": nrm(ks[18], (L, D_MODEL, D_MODEL), D_MODEL),
        "norm_ffn": gain(ks[19], (L, D_MODEL)),
        "w_ffn_gate": nrm(ks[20], (L, D_MODEL, D_FF), D_MODEL),
        "w_ffn_up": nrm(ks[21], (L, D_MODEL, D_FF), D_MODEL),
        "w_ffn_down": nrm(ks[22], (L, D_FF, D_MODEL), D_FF),
    }


def reference(x, norm_attn, w_in, fox_f_bias, fox_q_gain, fox_k_gain,
              nsa_q_gain, nsa_kc_gain, nsa_ks_gain, nsa_kw_gain,
              cmp_pe_k, cmp_w1_k, cmp_w2_k, cmp_pe_v, cmp_w1_v, cmp_w2_v,
              w_up_fox, w_up_nsa, w_out, norm_ffn, w_ffn_gate, w_ffn_up, w_ffn_down):
    B, T, _ = x.shape
    points = tuple(int(p) for p in np.cumsum(IN_SPLITS)[:-1])
    for l in range(DEPTH):
        xn = _rms(x, norm_attn[l])
        (fq, fk, fv, f_logit, nq, kc, vc, ksl, vsl, kw, vw,
         nsa_gate, gate_a, gate_b) = jnp.split(xn @ w_in[l], points, axis=-1)

        fq = _rms(fq.reshape(B, T, FOX_HEADS, FOX_HEAD_DIM), fox_q_gain[l])
        fk = _rms(fk.reshape(B, T, FOX_HEADS, FOX_HEAD_DIM), fox_k_gain[l])
        fv = fv.reshape(B, T, FOX_HEADS, FOX_HEAD_DIM)
        log_f = jax.nn.log_sigmoid(f_logit.astype(jnp.float32) + fox_f_bias[l].astype(jnp.float32))
        o_a = _fox_attention(fq, fk, fv, log_f)

        o_b = _nsa_attention(nq, kc, vc, ksl, vsl, kw, vw, nsa_gate,
                             nsa_q_gain[l], nsa_kc_gain[l], nsa_ks_gain[l], nsa_kw_gain[l],
                             cmp_pe_k[l], cmp_w1_k[l], cmp_w2_k[l],
                             cmp_pe_v[l], cmp_w1_v[l], cmp_w2_v[l])

        merged = (jax.nn.sigmoid(gate_a) * (o_a @ w_up_fox[l])
                  + jax.nn.sigmoid(gate_b) * (o_b @ w_up_nsa[l]))
        x = x + (merged @ w_out[l]).astype(x.dtype)

        hn = _rms(x, norm_ffn[l])
        x = x + ((jax.nn.silu(hn @ w_ffn_gate[l]) * (hn @ w_ffn_up[l])) @ w_ffn_down[l]).astype(x.dtype)
    return x
```

```python
from contextlib import ExitStack
import numpy as np
import concourse.bass as bass
import concourse.mybir as mybir
from concourse.bass_utils import run_bass_kernel_spmd

F32 = mybir.dt.float32
BF16 = mybir.dt.bfloat16
ALU = mybir.AluOpType
AF = mybir.ActivationFunctionType

ENGS = ("pe", "act", "dve", "pool", "sp")
NEG = -30000.0
D = 2048
T = 2048
TO = 1024
DFF = 5632
C_FQ, C_FK, C_FV, C_FL, C_NQ, C_KC, C_VC, C_KS, C_VS, C_KW, C_VW, C_NG, C_GA, C_GB = (
    0, 1024, 2048, 3072, 3080, 4616, 5000, 5256, 5640, 5896, 6280, 6536, 6560, 8608)
DIN = 10656


class _Op:
    __slots__ = ("eng", "fn", "deps", "idx", "dma_key", "signaled", "count", "waits")

    def __init__(self, eng, fn, dma_key):
        self.eng = eng
        self.fn = fn
        self.deps = []
        self.dma_key = dma_key
        self.signaled = False
        self.count = 0
        self.waits = []


class Prog:
    def __init__(self, nc):
        self.nc = nc
        self.ops = {e: [] for e in ENGS}
        self.lastw = {}
        self.readers = {}
        self.dma_cnt = {}
        self.dma_last = {}
        self.bar = []

    def op(self, eng, fn, reads=(), writes=(), dma=None):
        o = _Op(eng, fn, dma)
        deps = list(self.bar)
        for b in reads:
            w = self.lastw.get(b)
            if w is not None:
                deps.append(w)
        for b in writes:
            w = self.lastw.get(b)
            if w is not None:
                deps.append(w)
            deps.extend(self.readers.get(b, ()))
        seen = set()
        for d in deps:
            if id(d) in seen:
                continue
            if d.eng == "pe" and eng == "pe" and d.dma_key is None and dma is None:
                continue
            seen.add(id(d))
            o.deps.append(d)
        for b in reads:
            self.readers.setdefault(b, []).append(o)
        for b in writes:
            self.lastw[b] = o
            self.readers[b] = []
        o.idx = len(self.ops[eng])
        self.ops[eng].append(o)
        return o

    def dma(self, eng, fn, key, reads=(), writes=()):
        o = self.op(eng, fn, reads, writes, dma=key)
        self.dma_cnt[key] = self.dma_cnt.get(key, 0) + 16
        o.count = self.dma_cnt[key]
        self.dma_last[key] = o
        return o

    def barrier(self):
        b = []
        for e in ENGS:
            for o in reversed(self.ops[e]):
                if o.dma_key is None:
                    b.append(o)
                    break
        b.extend(self.dma_last.values())
        self.bar = b
        self.lastw = {}
        self.readers = {}

    def finalize(self, stack):
        nc = self.nc
        for e in ENGS:
            known = {}
            for o in self.ops[e]:
                need = {}
                for d in o.deps:
                    if d.dma_key is not None:
                        k, v = ("dma", d.dma_key), d.count
                    else:
                        k, v = d.eng, d.idx
                    if known.get(k, -1) >= v:
                        continue
                    if need.get(k, (-1, None))[0] < v:
                        need[k] = (v, d)
                for k, (v, d) in need.items():
                    known[k] = v
                    d.signaled = True
                    o.waits.append(d)
        for e in ENGS:
            c = 0
            for o in self.ops[e]:
                if o.dma_key is None and o.signaled:
                    c += 1
                    o.count = c
        self.sems = {e: stack.enter_context(nc.semaphore("sem_" + e)) for e in ENGS}
        self.dsems = {}
        for k in self.dma_cnt:
            self.dsems[k] = stack.enter_context(nc.semaphore("dsem_%d" % len(self.dsems)))

    def emit(self, eng, h):
        for o in self.ops[eng]:
            for d in o.waits:
                if d.dma_key is not None:
                    h.wait_ge(self.dsems[d.dma_key], d.count)
                else:
                    h.wait_ge(self.sems[d.eng], d.count)
            ins = o.fn(h)
            if o.dma_key is not None:
                ins.then_inc(self.dsems[o.dma_key], 16)
            elif o.signaled:
                ins.then_inc(self.sems[eng], 1)

    def run(self, stack, out_keys):
        self.finalize(stack)
        block = stack.enter_context(self.nc.Block())
        P = self

        @block.tensor
        def _(h):
            P.emit("pe", h)

        @block.scalar
        def _(h):
            P.emit("act", h)

        @block.vector
        def _(h):
            P.emit("dve", h)

        @block.gpsimd
        def _(h):
            P.emit("pool", h)

        @block.sync
        def _(h):
            P.emit("sp", h)
            for k in out_keys:
                h.wait_ge(P.dsems[k], P.dma_cnt[k])


class Arena:
    def __init__(self, nc, cap):
        self.nc = nc
        self.off = 0
        self.base = 16512
        self.cap = cap
        self.n = 0
        self.peak = 0

    def alloc(self, shape, dt):
        per = int(np.prod(shape[1:])) * (2 if dt == BF16 else 4)
        per = (per + 63) // 64 * 64
        assert self.off + per <= self.cap, ("SBUF arena overflow", self.off, per, self.cap)
        t = self.nc.alloc_sbuf_tensor_at("a%d" % self.n, list(shape), dt, offset=self.base + self.off)
        self.n += 1
        self.off += per
        self.peak = max(self.peak, self.off)
        return t

    def mark(self):
        return self.off

    def release(self, m):
        self.off = m


def _slopes():
    return [2.0 ** (-(i + 1)) for i in range(8)]


def _const_tables(h):
    c = {}
    c["idn"] = np.eye(128, dtype=np.float32)
    kl = np.arange(128)[:, None]
    ql = np.arange(512)[None, :]
    masks = np.zeros((12, 128, 512), np.float32)
    for m in range(4):
        masks[m] = np.where(ql >= 128 * m + kl, 0.0, NEG)
    for o in range(8):
        dist = ql - (128 * o - 512 + kl)
        masks[4 + o] = np.where((dist >= 0) & (dist < 512), 0.0, NEG)
    c["masks"] = masks.transpose(1, 0, 2).copy()
    n = np.arange(128)[:, None]
    p = 1024 + np.arange(1024)[None, :]
    dist = p - (16 * n + 31)
    ok = (dist >= 0) & (n < 127)
    if h == 0:
        ok &= n >= 64
    c["dc"] = np.where(ok, -dist, -1.0e9).astype(np.float32)
    j = np.arange(32)[None, :]
    pp = 1024 + np.arange(1024)[:, None]
    cur = pp // 64
    first = 0 if h == 1 else 16
    elig = (j >= first) & (j <= cur)
    forced = elig & ((j == first) | (cur - j < 2))
    A = (elig & ~forced).astype(np.float32)
    B = np.where(forced, 1.0e4 + j, np.where(elig, 0.0, -1.0)).astype(np.float32)
    c["selA"] = A.reshape(8, 128, 32).transpose(1, 0, 2).copy()
    c["selB"] = B.reshape(8, 128, 32).transpose(1, 0, 2).copy()
    lo = np.maximum(n * 16, j * 64)
    hi = np.minimum(n * 16 + 32, (j + 1) * 64)
    ov = (np.maximum(hi - lo, 0) / 16).astype(np.float32)
    ov[127] = 0
    c["ov"] = ov
    k = np.arange(2048)
    ka = np.zeros((37, 2048), np.float32)
    ka[k // 64, k] = 1.0
    ka[32] = 1.0
    ka[33] = 1.0
    ka[34] = 128.0 * (k // 128)
    ka[35] = k % 128
    kw = ka.copy()
    kw[0:32] = 0.0
    kw[36] = np.where((k < 1024) & (h == 0), NEG, 0.0)
    c["kaug"] = ka
    c["kaugw"] = kw
    q = 1024 + np.arange(1024)
    qa = np.zeros((5, 8, 1024), np.float32)
    for hh, s in enumerate(_slopes()):
        qa[0, hh] = -s * 128.0 * (q // 128)
        qa[1, hh] = -s * (q % 128)
        qa[2, hh] = s
        qa[3, hh] = s
        qa[4, hh] = 1.0
    c["qaug"] = qa
    invk = np.where((np.arange(2048) < 1024) & (h == 0), NEG, 0.0).astype(np.float32)
    c["invk"] = invk.reshape(16, 128).T.copy()
    return c


def build():
    nc = bass.Bass("TRN2", target_bir_lowering=False)
    st = ExitStack()
    din = lambda name, shape: nc.dram_tensor(name, list(shape), F32, kind="ExternalInput").ap()
    x_d = din("x", [T, D])
    w_in = din("w_in", [D, DIN])
    norm_attn = din("norm_attn", [1, D])
    fbias = din("fox_f_bias", [1, 8])
    g_fq = din("fox_q_gain", [1, 128])
    g_fk = din("fox_k_gain", [1, 128])
    g_nq = din("nsa_q_gain", [1, 192])
    g_kc = din("nsa_kc_gain", [1, 192])
    g_ks = din("nsa_ks_gain", [1, 192])
    g_kw = din("nsa_kw_gain", [1, 192])
    pe_k = din("cmp_pe_k", [32, 192])
    w1_k = din("cmp_w1_k", [6144, 256])
    w2_k = din("cmp_w2_k", [256, 192])
    pe_v = din("cmp_pe_v", [32, 128])
    w1_v = din("cmp_w1_v", [4096, 256])
    w2_v = din("cmp_w2_v", [256, 128])
    w_upf = din("w_up_fox", [1024, D])
    w_upn = din("w_up_nsa", [1024, D])
    w_out = din("w_out", [D, D])
    norm_ffn = din("norm_ffn", [1, D])
    w_g = din("w_ffn_gate", [D, DFF])
    w_u = din("w_ffn_up", [D, DFF])
    w_d = din("w_ffn_down", [DFF, D])
    c_idn = din("c_idn", [128, 128])
    c_masks = din("c_masks", [128, 12, 512])
    c_dc = din("c_dc", [128, 1024])
    c_selA = din("c_selA", [128, 8, 32])
    c_selB = din("c_selB", [128, 8, 32])
    c_ov = din("c_ov", [128, 32])
    c_kaug = din("c_kaug", [37, 2048])
    c_kaugw = din("c_kaugw", [37, 2048])
    c_qaug = din("c_qaug", [5, 8, 1024])
    c_invk = din("c_invk", [128, 16])
    y_d = nc.dram_tensor("y", [TO, D], F32, kind="ExternalOutput").ap()
    h_scr = nc.dram_tensor("h_scr", [TO, D], F32, kind="Internal").ap()
    oa_scr = nc.dram_tensor("oa_scr", [128, 8, TO], BF16, kind="Internal").ap()

    P = Prog(nc)
    A = Arena(nc, 229344 - 16512)
    uid = [0]

    def K(s):
        uid[0] += 1
        return "%s#%d" % (s, uid[0])

    pb = [st.enter_context(nc.psum_tensor("pb%d" % i, [128, 512], F32)) for i in range(7)]
    pbt = st.enter_context(nc.psum_tensor("pbt", [128, 1024], BF16))
    PB = ["pb%d" % i for i in range(7)]

    def mm(out, lhsT, rhs, start, stop, r, w, sgc=False):
        P.op("pe", lambda h: h.matmul(out, lhsT=lhsT, rhs=rhs, start=start, stop=stop, skip_group_check=sgc), reads=r, writes=w)

    def tr(out, in_, ident, r, w):
        P.op("pe", lambda h: h.transpose(out=out, in_=in_, identity=ident), reads=r, writes=w)

    def act(out, in_, func, r, w, bias=None, scale=None, accum=None, eng="act"):
        kw = {}
        if bias is not None:
            kw["bias"] = bias
        if scale is not None:
            kw["scale"] = scale
        if accum is not None:
            kw["accum_out"] = accum
        P.op("act", lambda h: h.activation(out=out, in_=in_, func=func, **kw), reads=r, writes=w)

    def vcopy(out, in_, r, w, eng="dve"):
        P.op(eng, lambda h: h.tensor_copy(out=out, in_=in_), reads=r, writes=w)

    def vtt(out, in0, in1, op, r, w, eng="dve"):
        P.op(eng, lambda h: h.tensor_tensor(out=out, in0=in0, in1=in1, op=op), reads=r, writes=w)

    def vts(out, in0, s1, s2, op0, op1, r, w, eng="dve"):
        if op1 is None:
            P.op(eng, lambda h: h.tensor_scalar(out=out, in0=in0, scalar1=s1, scalar2=None, op0=op0), reads=r, writes=w)
        else:
            P.op(eng, lambda h: h.tensor_scalar(out=out, in0=in0, scalar1=s1, scalar2=s2, op0=op0, op1=op1), reads=r, writes=w)

    def vstt(out, in0, sc, in1, op0, op1, r, w, eng="dve"):
        P.op(eng, lambda h: h.scalar_tensor_tensor(out=out, in0=in0, scalar=sc, in1=in1, op0=op0, op1=op1), reads=r, writes=w)

    def vrecip(out, in_, r, w):
        P.op("dve", lambda h: h.reciprocal(out=out, in_=in_), reads=r, writes=w)

    def memset(ap, v, w, eng="pool"):
        P.op(eng, lambda h: h.memset(ap, v), writes=w)

    def dma(out, in_, key, r=(), w=(), eng="sp"):
        P.dma(eng, lambda h: h.dma_start(out=out, in_=in_), key, reads=r, writes=w)

    idf = A.alloc([128, 128], F32)
    idb = A.alloc([128, 128], BF16)
    onesb = A.alloc([128, 128], BF16)
    zerob = A.alloc([128, 512], BF16)
    masks = A.alloc([128, 12, 512], BF16)
    epsc = A.alloc([128, 1], F32)
    onec = A.alloc([128, 1], F32)
    invk = A.alloc([128, 16], F32)
    wr = [A.alloc([128, 16, 256], BF16) for _ in range(2)]
    mTop = A.mark()
    o_bT = A.alloc([128, 8, TO], BF16)
    xnT = A.alloc([128, 16, T], BF16)
    WR = ["wr0", "wr1"]
    wri = [0]

    dma(idf[:], c_idn[:, :], K("c"), w=["idf"])
    dma(invk[:], c_invk[:, :], K("c"), w=["invk"])
    vcopy(idb[:], idf[:], ["idf"], ["idb"])
    memset(onesb[:], 1.0, ["onesb"])
    memset(zerob[:], 0.0, ["zerob"])
    memset(epsc[:], 1e-6, ["epsc"])
    memset(onec[:], 1.0, ["onec"])
    P.dma("pool", lambda h: h.dma_start(out=masks[:], in_=c_masks[:, :, :]), "c1", writes=["masks"])

    P.barrier()

    def wload(src3):
        s = wri[0] % 2
        wri[0] += 1
        k, n = src3.shape[1], src3.shape[2]
        dst = wr[s][:, 0:k, 0:n]
        P.dma("pool", lambda h: h.dma_start(out=dst, in_=src3), "wr%d" % s, writes=[WR[s]])
        return wr[s], WR[s]

    def win_cols(c0, n):
        return w_in[:, c0:c0 + n].rearrange("(c p) n -> p c n", p=128)

    def rstd_from_ssq(ssq_ps, np_, n, dim, r, tmp, tmpk):
        act(tmp[0:np_, 0:n], ssq_ps, AF.Sqrt, r, [tmpk], bias=epsc[0:np_, :], scale=1.0 / dim)
        vrecip(tmp[0:np_, 0:n], tmp[0:np_, 0:n], [tmpk], [tmpk])

    mA = A.mark()
    gbc = A.alloc([128, D], F32)
    xt = [A.alloc([128, D], F32) for _ in range(2)]
    xn = [A.alloc([128, D], BF16) for _ in range(2)]
    junk = A.alloc([128, D], BF16)
    st8 = A.alloc([128, 16], F32)
    dma(gbc[:], norm_attn[0:1, :].partition_broadcast(128), K("c"), w=["gbc"])

    def rms_tile(src_rows, t, gb, gbk, dstT, dstk, col0):
        s = t % 2
        dma(xt[s][:], src_rows, "xt%d" % s, w=["xt%d" % s])
        act(junk[:], xt[s][:], AF.Square, ["xt%d" % s], ["junk", "ssq%d" % s], accum=st8[:, s:s + 1])
        act(st8[:, 2 + s:3 + s], st8[:, s:s + 1], AF.Sqrt, ["ssq%d" % s], ["rs%d" % s], bias=epsc[:], scale=1.0 / D)
        vrecip(st8[:, 2 + s:3 + s], st8[:, 2 + s:3 + s], ["rs%d" % s], ["rs%d" % s])
        vstt(xn[s][:], xt[s][:], st8[:, 2 + s:3 + s], gb[:], ALU.mult, ALU.mult, ["xt%d" % s, "rs%d" % s, gbk], ["xn%d" % s])
        for half in range(2):
            for c in range(8):
                cc = half * 8 + c
                tr(pbt[:, c * 128:(c + 1) * 128], xn[s][:, cc * 128:(cc + 1) * 128], idb[:], ["xn%d" % s, "idb"], ["pbt"])
            dst = dstT[:, half * 8:half * 8 + 8, col0:col0 + 128]
            src = pbt[:, :].rearrange("p (c n) -> p c n", c=8)
            if half == 0:
                P.op("act", lambda h, dst=dst, src=src: h.copy(out=dst, in_=src), reads=["pbt"], writes=[dstk])
            else:
                vcopy(dst, src, ["pbt"], [dstk])

    for t in range(16):
        rms_tile(x_d[t * 128:(t + 1) * 128, :], t, gbc, "gbc", xnT, "xnT", t * 128)
    A.release(mA)
    P.barrier()

    def proj_fm(ps, psk, wt, wk, wc0, m, tok0, ntok):
        for c in range(16):
            mm(ps[0:m, 0:ntok], wt[:, c, wc0:wc0 + m], xnT[:, c, tok0:tok0 + ntok], c == 0, c == 15, [wk, "xnT"], [psk])

    def proj_tm(ps, psk, wt, wk, wc0, n, tile):
        for c in range(16):
            mm(ps[:, 0:n], xnT[:, c, tile * 128:(tile + 1) * 128], wt[:, c, wc0:wc0 + n], c == 0, c == 15, [wk, "xnT"], [psk])

    mB = A.mark()
    o_aT = A.alloc([128, 8, TO], BF16)
    gq = A.alloc([128, 1], F32)
    gk = A.alloc([128, 1], F32)
    fb = A.alloc([8, 1], F32)
    bfox = A.alloc([128, 16, 2, 8], F32)
    mF = A.mark()
    cfm = A.alloc([8, T], F32)
    sp = A.alloc([8, T], F32)
    one8 = A.alloc([8, T], F32)
    dfm = A.alloc([8, 2, T], F32)
    dma(gq[:], g_fq.rearrange("o d -> d o"), K("c"), w=["gq"])
    dma(gk[:], g_fk.rearrange("o d -> d o"), K("c"), w=["gk"])
    dma(fb[:], fbias.rearrange("o d -> d o"), K("c"), w=["fb"])
    vts(gq[:], gq[:], float(128 ** -0.5), None, ALU.mult, None, ["gq"], ["gq"])
    memset(one8[:], 1.0, ["one8"])
    wt, wk = wload(win_cols(C_FL, 8))
    for tc in range(4):
        proj_fm(pb[0], PB[0], wt, wk, 0, 8, tc * 512, 512)
        vts(sp[:, tc * 512:(tc + 1) * 512], pb[0][0:8, :], fb[:, 0:1], -1.0, ALU.add, ALU.mult, [PB[0], "fb"], ["sp"])
    act(sp[:], sp[:], AF.Exp, ["sp"], ["sp"])
    act(sp[:], sp[:], AF.Ln, ["sp"], ["sp"], bias=onec[0:8, :])
    vts(sp[:], sp[:], -1.0, None, ALU.mult, None, ["sp"], ["sp"])
    P.op("dve", lambda h: h.tensor_tensor_scan(out=cfm[:], data0=one8[:], data1=sp[:], initial=0.0, op0=ALU.mult, op1=ALU.add),
         reads=["one8", "sp"], writes=["cfm"])
    for j in range(2):
        e = 1024 + 512 * (j + 1) - 1
        vts(dfm[:, j, :], cfm[:], -1.0, cfm[:, e:e + 1], ALU.mult, ALU.add, ["cfm"], ["dfm"])
    for j in range(2):
        for t in range(16):
            b = pb[1 + (t % 2)]
            bk = PB[1 + (t % 2)]
            tr(b[:, 0:8], dfm[:, j, t * 128:(t + 1) * 128], idf[0:8, 0:8], ["dfm", "idf"], [bk])
            vtt(bfox[:, t, j, :], b[:, 0:8], invk[:, t:t + 1].to_broadcast([128, 8]), ALU.add, [bk, "invk"], ["bfox"])

    P.barrier()
    A.release(mF)
    sqb = [A.alloc([128, 512], BF16) for _ in range(2)]
    rsb = [A.alloc([128, 512], F32) for _ in range(2)]
    ptb = [A.alloc([128, 512], BF16) for _ in range(2)]
    rzb = A.alloc([128, 8], F32)
    cnt = {"n": 0, "pt": 0}

    def headnorm(ps_lo, pk_lo, ps_hi, pk_hi, dim, gl, gh, dst_lo, dst_hi, dk, n):
        s = cnt["n"] % 2
        cnt["n"] += 1
        sk, rk = "sq%d" % s, "rs_%d" % s
        act(sqb[s][:, 0:n], ps_lo[:, 0:n], AF.Square, [pk_lo], [sk])
        mm(pb[6][:, 0:n], onesb[:, :], sqb[s][:, 0:n], True, ps_hi is None, [sk, "onesb"], [PB[6]])
        if ps_hi is not None:
            sk2 = sk + "h"
            act(sqh[s][0:64, 0:n], ps_hi[0:64, 0:n], AF.Square, [pk_hi], [sk2])
            mm(pb[6][:, 0:n], onesb[0:64, :], sqh[s][0:64, 0:n], False, True, [sk2, "onesb"], [PB[6]])
        rstd_from_ssq(pb[6][:, 0:n], 128, n, dim, [PB[6]], rsb[s], rk)
        vstt(dst_lo, ps_lo[:, 0:n], gl[:, 0:1], rsb[s][:, 0:n], ALU.mult, ALU.mult, [pk_lo, rk], [dk])
        if ps_hi is not None:
            vstt(dst_hi, ps_hi[0:64, 0:n], gh[0:64, 0:1], rsb[s][0:64, 0:n], ALU.mult, ALU.mult, [pk_hi, rk], [dk])

    def attn_block(S_terms, nkt, diag_of, bias_of, vaug_of, skip_of, po, pok, out_fn, rdeps):
        for b in range(2):
            mm(po[b][:, 0:258], zerob[:, 0:128], zerob[:, 0:258], True, False, ["zerob"], [pok[b]], sgc=True)
        for kt in nkt:
            s = cnt["pt"] % 2
            cnt["pt"] += 1
            ps, psk = pb[s], PB[s]
            terms = S_terms(kt)
            dg = diag_of(kt)
            if dg is not None:
                terms = terms + [(idb[:, :], masks[:, dg, :])]
            for i, (l, r_) in enumerate(terms):
                mm(ps[:, :], l, r_, i == 0, i == len(terms) - 1, rdeps + ["idb", "masks"], [psk])
            bi = bias_of(kt)
            if bi is None:
                act(ptb[s][:], ps[:, :], AF.Exp, [psk], ["pt%d" % s])
            else:
                act(ptb[s][:], ps[:, :], AF.Exp, [psk, "bfox"], ["pt%d" % s], bias=bi)
            va = vaug_of(kt)
            for qb in range(4):
                if skip_of(kt, qb):
                    continue
                mm(po[qb // 2][:, (qb % 2) * 129:(qb % 2) * 129 + 129], ptb[s][:, qb * 128:(qb + 1) * 128], va, False, False,
                   ["pt%d" % s] + rdeps, [pok[qb // 2]], sgc=True)
        for qb in range(4):
            o = po[qb // 2][:, (qb % 2) * 129:(qb % 2) * 129 + 129]
            vts(rzb[:, qb:qb + 1], o[:, 128:129], 1e-30, None, ALU.max, None, [pok[qb // 2]], ["rz%d" % qb])
            vrecip(rzb[:, qb:qb + 1], rzb[:, qb:qb + 1], ["rz%d" % qb], ["rz%d" % qb])
            out_fn(qb, o[:, 0:128], pok[qb // 2], rzb[:, qb:qb + 1], "rz%d" % qb)

    def own_tiles(j):
        return list(range(0, 8 + 4 * (j + 1)))

    for hg in range(2):
        mH = A.mark()
        qT = A.alloc([128, 4, TO], BF16)
        kT = A.alloc([128, 4, T], BF16)
        vaug = A.alloc([128, 16, 4, 129], BF16)
        oa = A.alloc([128, 8, 4, 128], BF16)
        memset(vaug[:, :, :, 128:129], 1.0, ["vaug"])
        for hp in range(2):
            wt, wk = wload(win_cols(C_FK + (hg * 4 + hp * 2) * 128, 256))
            for hl in range(2):
                hi_ = hp * 2 + hl
                for tc in range(4):
                    b, bk = pb[2 + tc % 2], PB[2 + tc % 2]
                    proj_fm(b, bk, wt, wk, hl * 128, 128, tc * 512, 512)
                    headnorm(b, bk, None, None, 128, gk, None, kT[:, hi_, tc * 512:(tc + 1) * 512], None, "kT", 512)
            wt, wk = wload(win_cols(C_FQ + (hg * 4 + hp * 2) * 128, 256))
            for hl in range(2):
                hi_ = hp * 2 + hl
                for tc in range(2):
                    b, bk = pb[2 + tc % 2], PB[2 + tc % 2]
                    proj_fm(b, bk, wt, wk, hl * 128, 128, 1024 + tc * 512, 512)
                    headnorm(b, bk, None, None, 128, gq, None, qT[:, hi_, tc * 512:(tc + 1) * 512], None, "qT", 512)
            wt, wk = wload(win_cols(C_FV + (hg * 4 + hp * 2) * 128, 256))
            for t in range(16):
                b, bk = pb[2 + t % 2], PB[2 + t % 2]
                proj_tm(b, bk, wt, wk, 0, 256, t)
                vcopy(vaug[:, t, hp * 2:hp * 2 + 2, 0:128], b[:, 0:256].rearrange("p (h d) -> p h d", h=2), [bk], ["vaug"])
        for hi_ in range(4):
            hh = hg * 4 + hi_
            for j in range(2):
                def S_terms(kt, hi_=hi_, j=j):
                    return [(kT[:, hi_, kt * 128:(kt + 1) * 128], qT[:, hi_, j * 512:(j + 1) * 512])]

                def out_fn(qb, o, ok, rz, rzk, hi_=hi_, j=j):
                    vts(oa[:, j * 4 + qb, hi_, :], o, rz, None, ALU.mult, None, [ok, rzk], ["oa"])

                attn_block(S_terms, own_tiles(j),
                           lambda kt, j=j: (kt - 8 - 4 * j) if kt >= 8 + 4 * j else None,
                           lambda kt, j=j, hh=hh: bfox[:, kt, j, hh:hh + 1],
                           lambda kt, hi_=hi_: vaug[:, kt, hi_, :],
                           lambda kt, qb, j=j: kt - 8 > 4 * j + qb,
                           [pb[4], pb[5]], [PB[4], PB[5]], out_fn, ["kT", "qT", "vaug"])
        for hi_ in range(4):
            for qt in range(8):
                tr(pbt[:, qt * 128:(qt + 1) * 128], oa[:, qt, hi_, :], idb[:], ["oa", "idb"], ["pbt"])
            vcopy(o_aT[:, hg * 4 + hi_, :], pbt[:, :], ["pbt"], ["o_aT"])
        A.release(mH)
        P.barrier()
    dma(oa_scr[:, :, :], o_aT[:], "oas", r=["o_aT"], w=["oa_scr"])
    P.barrier()
    A.release(mB)

    mC = A.mark()
    kaug = A.alloc([37, T], BF16)
    kaugw = A.alloc([37, T], BF16)
    dc = A.alloc([128, TO], F32)
    selA = A.alloc([128, 8, 32], F32)
    selB = A.alloc([128, 8, 32], F32)
    gts = A.alloc([128, 8, 24], F32)
    gnq = A.alloc([128, 2], F32)
    gkc = A.alloc([128, 2], F32)
    gks = A.alloc([128, 2], F32)
    gkw = A.alloc([128, 2], F32)
    sqb = [A.alloc([128, 512], BF16) for _ in range(2)]
    rsb = [A.alloc([128, 512], F32) for _ in range(2)]
    ptb = [A.alloc([128, 512], BF16) for _ in range(2)]
    tfb = [A.alloc([128, 512], F32) for _ in range(2)]
    sqh = [A.alloc([64, 512], BF16) for _ in range(2)]
    rzb = A.alloc([128, 8], F32)
    P.dma("pool", lambda h: h.dma_start(out=kaug[:], in_=c_kaug[:, :]), K("c"), writes=["kaug"])
    P.dma("pool", lambda h: h.dma_start(out=kaugw[:], in_=c_kaugw[:, :]), K("c"), writes=["kaugw"])
    dma(dc[:], c_dc[:, :], K("c"), w=["dc"])
    dma(selA[:], c_selA[:, :, :], K("c"), w=["selA"])
    dma(selB[:], c_selB[:, :, :], K("c"), w=["selB"])
    for gi_, (gt_, gd) in enumerate(((gnq, g_nq), (gkc, g_kc), (gks, g_ks), (gkw, g_kw))):
        memset(gt_[:], 1.0, ["gg%d" % gi_])
        dma(gt_[:, 0:1], gd[0:1, 0:128].rearrange("o d -> d o"), K("c"), w=["gg%d" % gi_])
        dma(gt_[0:64, 1:2], gd[0:1, 128:192].rearrange("o d -> d o"), K("c"), w=["gg%d" % gi_])
    P.barrier()
    vts(gnq[:], gnq[:], float(192 ** -0.5), None, ALU.mult, None, [], ["gnq"])
    wt, wk = wload(win_cols(C_NG, 24))
    for qt in range(8):
        b, bk = pb[2 + qt % 2], PB[2 + qt % 2]
        proj_tm(b, bk, wt, wk, 0, 24, 8 + qt)
        act(gts[:, qt, :], b[:, 0:24], AF.Exp, [bk], ["gts"], scale=-1.0)
    vts(gts[:], gts[:], 1.0, None, ALU.add, None, ["gts"], ["gts"])
    vrecip(gts[:], gts[:], ["gts"], ["gts"])
    slopes = _slopes()

    for g in range(2):
        mg = A.mark()
        nq_lo = A.alloc([128, 4, TO], BF16)
        nq_hi = A.alloc([64, 4, TO], BF16)
        qaug = A.alloc([37, 4, TO], BF16)
        ks_lo = A.alloc([128, T], BF16)
        ks_hi = A.alloc([64, T], BF16)
        kw_lo = A.alloc([128, T], BF16)
        kw_hi = A.alloc([64, T], BF16)
        vsa = A.alloc([128, 16, 129], BF16)
        vwa = A.alloc([128, 16, 129], BF16)
        imp = A.alloc([128, 8, 32], F32)
        sc = A.alloc([128, 32], F32)
        sc2 = A.alloc([128, 32], F32)
        m8 = A.alloc([128, 16], F32)
        selb = A.alloc([128, 32], BF16)
        kc_lo = A.alloc([128, 128], BF16)
        kc_hi = A.alloc([64, 128], BF16)
        R = A.alloc([128, 161], BF16)
        aT = A.alloc([128, 2, 128], BF16)
        hb = A.alloc([128, 2], F32)
        peT = A.alloc([128, 2, 32], BF16)
        pest = A.alloc([32, 192], F32)
        w2b = A.alloc([128, 2, 192], BF16)
        et = A.alloc([128, 128], F32)
        mz = A.mark()
        zk_lo = A.alloc([128, T], BF16)
        zk_hi = A.alloc([64, T], BF16)
        zv = A.alloc([128, T], BF16)
        memset(vsa[:, :, 128:129], 1.0, ["vsa"])
        memset(vwa[:, :, 128:129], 1.0, ["vwa"])
        memset(R[:, :], 0.0, ["R"])
        memset(R[:, 32:33], 1.0, ["R"])
        P.dma("pool", lambda h, R=R: h.dma_start(out=R[:, 0:32], in_=c_ov[:, :]), K("c"), writes=["R"])
        P.dma("pool", lambda h, qaug=qaug, g=g: h.dma_start(out=qaug[32:37, :, :], in_=c_qaug[:, 4 * g:4 * g + 4, :]), K("c"), writes=["qaug"])
        memset(kc_lo[:], 0.0, ["kc"])
        memset(kc_hi[:], 0.0, ["kc"])

        for hl in range(4):
            c0 = C_NQ + (4 * g + hl) * 192
            wt, wk = wload(win_cols(c0, 192))
            for tc in range(2):
                proj_fm(pb[2], PB[2], wt, wk, 0, 128, 1024 + tc * 512, 512)
                proj_fm(pb[3], PB[3], wt, wk, 128, 64, 1024 + tc * 512, 512)
                headnorm(pb[2], PB[2], pb[3], PB[3], 192, gnq[:, 0:1], gnq[:, 1:2],
                         nq_lo[:, hl, tc * 512:(tc + 1) * 512], nq_hi[0:64, hl, tc * 512:(tc + 1) * 512], "nq", 512)
        for (cb, dlo, dhi, gg, dk) in ((C_KS, ks_lo, ks_hi, gks, "ks"), (C_KW, kw_lo, kw_hi, gkw, "kw")):
            wt, wk = wload(win_cols(cb + g * 192, 192))
            for tc in range(4):
                proj_fm(pb[2], PB[2], wt, wk, 0, 128, tc * 512, 512)
                proj_fm(pb[3], PB[3], wt, wk, 128, 64, tc * 512, 512)
                headnorm(pb[2], PB[2], pb[3], PB[3], 192, gg[:, 0:1], gg[:, 1:2],
                         dlo[:, tc * 512:(tc + 1) * 512], dhi[0:64, tc * 512:(tc + 1) * 512], dk, 512)
        wt, wk = wload(win_cols(C_KC + g * 192, 192))
        for tc in range(4):
            proj_fm(pb[2], PB[2], wt, wk, 0, 128, tc * 512, 512)
            vcopy(zk_lo[:, tc * 512:(tc + 1) * 512], pb[2][:, :], [PB[2]], ["zk"])
            proj_fm(pb[3], PB[3], wt, wk, 128, 64, tc * 512, 512)
            vcopy(zk_hi[0:64, tc * 512:(tc + 1) * 512], pb[3][0:64, :], [PB[3]], ["zk"])
        wt, wk = wload(win_cols(C_VC + g * 128, 128))
        for tc in range(4):
            proj_fm(pb[2 + tc % 2], PB[2 + tc % 2], wt, wk, 0, 128, tc * 512, 512)
            vcopy(zv[:, tc * 512:(tc + 1) * 512], pb[2 + tc % 2][:, :], [PB[2 + tc % 2]], ["zv"])
        for (cb, dst, dk) in ((C_VS, vsa, "vsa"), (C_VW, vwa, "vwa")):
            wt, wk = wload(win_cols(cb + g * 128, 128))
            for t in range(16):
                b, bk = pb[2 + t % 2], PB[2 + t % 2]
                proj_tm(b, bk, wt, wk, 0, 128, t)
                vcopy(dst[:, t, 0:128], b[:, 0:128], [bk], [dk])

        def compress(w1, w2, pe, dim, z_lo, z_hi, is_k):
            nch = 2 if is_k else 1
            w1r = w1.rearrange("(l d) j -> d l j", d=dim)
            dma(pest[:, 0:dim], pe[:, :], "c6", w=["pest"])
            tr(pb[2][:, 0:32], pest[:, 0:128], idf[0:32, 0:32], ["pest", "idf"], [PB[2]])
            vcopy(peT[:, 0, :], pb[2][:, 0:32], [PB[2]], ["peT"])
            if is_k:
                tr(pb[3][0:64, 0:32], pest[:, 128:192], idf[0:32, 0:32], ["pest", "idf"], [PB[3]])
                vcopy(peT[0:64, 1, :], pb[3][0:64, 0:32], [PB[3]], ["peT"])
            P.dma("pool", lambda h, w2b=w2b, dim=dim, w2=w2: h.dma_start(out=w2b[:, :, 0:dim], in_=w2.rearrange("(c p) n -> p c n", p=128)), "c7", writes=["w2b"])
            for lg in range(8):
                ws = []
                for ch in range(nch):
                    rows = 128 if ch == 0 else 64
                    src = w1r[ch * 128:ch * 128 + rows, lg * 4:lg * 4 + 4, :]
                    s = wri[0] % 2
                    wri[0] += 1
                    dst = wr[s][0:rows, 0:4, 0:256]
                    P.dma("pool", lambda h, dst=dst, src=src: h.dma_start(out=dst, in_=src), "wr%d" % s, writes=[WR[s]])
                    ws.append((wr[s], WR[s], rows))
                for li in range(4):
                    l = lg * 4 + li
                    for ch, (wt_, wk_, rows) in enumerate(ws):
                        z = z_lo if ch == 0 else z_hi
                        first = (lg == 0 and li == 0 and ch == 0)
                        last = (lg == 7 and li == 3 and ch == nch - 1)
                        for jc in range(2):
                            mm(pb[2 + jc][:, 0:127], wt_[0:rows, li, jc * 128:(jc + 1) * 128], z[0:rows, l:l + 16 * 126 + 1:16],
                               first, last, [wk_, "zk", "zv"], [PB[2 + jc]])
                            mm(pb[4 + jc][:, 0:1], wt_[0:rows, li, jc * 128:(jc + 1) * 128], peT[0:rows, ch, l:l + 1],
                               first, last, [wk_, "peT"], [PB[4 + jc]])
            for jc in range(2):
                vcopy(hb[:, jc:jc + 1], pb[4 + jc][:, 0:1], [PB[4 + jc]], ["hb"])
                vts(et[:, 0:127], pb[2 + jc][:, 0:127], hb[:, jc:jc + 1], None, ALU.add, None, [PB[2 + jc], "hb"], ["et"])
                act(tfb[0][:, 0:127], et[:, 0:127], AF.Exp, ["et"], ["tf0"], scale=-1.0)
                vts(tfb[0][:, 0:127], tfb[0][:, 0:127], 1.0, None, ALU.add, None, ["tf0"], ["tf0"])
                vrecip(tfb[0][:, 0:127], tfb[0][:, 0:127], ["tf0"], ["tf0"])
                vtt(aT[:, jc, 0:127], et[:, 0:127], tfb[0][:, 0:127], ALU.mult, ["et", "tf0"], ["aT"])
            if is_k:
                for jc in range(2):
                    mm(pb[2][:, 0:127], w2b[:, jc, 0:128], aT[:, jc, 0:127], jc == 0, jc == 1, ["w2b", "aT"], [PB[2]])
                for jc in range(2):
                    mm(pb[3][0:64, 0:127], w2b[:, jc, 128:192], aT[:, jc, 0:127], jc == 0, jc == 1, ["w2b", "aT"], [PB[3]])
                headnorm(pb[2], PB[2], pb[3], PB[3], 192, gkc[:, 0:1], gkc[:, 1:2], kc_lo[:, 0:127], kc_hi[0:64, 0:127], "kc", 127)
            else:
                for jc in range(2):
                    mm(pb[2][0:127, 0:128], aT[:, jc, 0:127], w2b[:, jc, 0:128], jc == 0, jc == 1, ["w2b", "aT"], [PB[2]])
                vcopy(R[0:127, 33:161], pb[2][0:127, 0:128], [PB[2]], ["R"])

        compress(w1_k, w2_k, pe_k, 192, zk_lo, zk_hi, True)
        compress(w1_v, w2_v, pe_v, 128, zv, None, False)
        P.barrier()
        A.release(mz)
        ob = A.alloc([128, 4, 4, 128], F32)
        obb = A.alloc([128, 4, 4, 128], BF16)

        for j in range(2):
            for hl in range(4):
                hh = 4 * g + hl
                s = cnt["pt"] % 2
                cnt["pt"] += 1
                ps, psk = pb[s], PB[s]
                mm(ps[:, :], kc_lo[:, :], nq_lo[:, hl, j * 512:(j + 1) * 512], True, False, ["kc", "nq"], [psk])
                mm(ps[:, :], kc_hi[0:64, :], nq_hi[0:64, hl, j * 512:(j + 1) * 512], False, True, ["kc", "nq"], [psk])
                vstt(tfb[s][:, :], dc[:, j * 512:(j + 1) * 512], float(slopes[hh]), ps[:, :], ALU.mult, ALU.add, ["dc", psk], ["tf%d" % s])
                act(ptb[s][:], tfb[s][:, :], AF.Exp, ["tf%d" % s], ["pt%d" % s])
                for qb in range(4):
                    qt = j * 4 + qb
                    po = pb[4 + qb // 2][:, (qb % 2) * 161:(qb % 2) * 161 + 161]
                    pk = PB[4 + qb // 2]
                    mm(po, ptb[s][:, qb * 128:(qb + 1) * 128], R[:, :], True, True, ["pt%d" % s, "R"], [pk])
                    rz, rzk = rzb[:, qb:qb + 1], "rz%d" % qb
                    vts(rz, po[:, 32:33], 1e-30, None, ALU.max, None, [pk], [rzk])
                    vrecip(rz, rz, [rzk], [rzk])
                    if hl == 0:
                        vts(imp[:, qt, :], po[:, 0:32], rz, None, ALU.mult, None, [pk, rzk], ["imp"])
                    else:
                        vstt(imp[:, qt, :], po[:, 0:32], rz, imp[:, qt, :], ALU.mult, ALU.add, [pk, rzk, "imp"], ["imp"])
                    vtt(rz, rz, gts[:, qt, hh * 3:hh * 3 + 1], ALU.mult, [rzk, "gts"], [rzk])
                    vts(ob[:, qb, hl, :], po[:, 33:161], rz, None, ALU.mult, None, [pk, rzk], ["ob"])
            for qb in range(4):
                qt = j * 4 + qb
                vtt(sc[:], imp[:, qt, :], selA[:, qt, :], ALU.mult, ["imp", "selA"], ["sc"])
                vtt(sc[:], sc[:], selB[:, qt, :], ALU.add, ["sc", "selB"], ["sc"])
                P.op("dve", lambda h, m8=m8, sc=sc: h.max(out=m8[:, 0:8], in_=sc[:]), reads=["sc"], writes=["m8"])
                P.op("dve", lambda h, m8=m8, sc=sc, sc2=sc2: h.match_replace(out=sc2[:], in_to_replace=m8[:, 0:8], in_values=sc[:], imm_value=-1.0e9),
                     reads=["sc", "m8"], writes=["sc2"])
                P.op("dve", lambda h, m8=m8, sc2=sc2: h.max(out=m8[:, 8:16], in_=sc2[:]), reads=["sc2"], writes=["m8"])
                vts(m8[:, 15:16], m8[:, 15:16], 0.0, None, ALU.max, None, ["m8"], ["m8"])
                vts(sc2[:], sc[:], m8[:, 15:16], None, ALU.is_ge, None, ["sc", "m8"], ["sc2"])
                vts(selb[:], sc2[:], 1.0, -NEG, ALU.subtract, ALU.mult, ["sc2"], ["selb"])
                tr(pbt[0:32, 0:128], selb[:, :], idb[:], ["selb", "idb"], ["pbt"])
                for hl in range(4):
                    vcopy(qaug[0:32, hl, qt * 128:(qt + 1) * 128], pbt[0:32, 0:128], ["pbt"], ["qaug"])
            for hl in range(4):
                hh = 4 * g + hl

                def S_sel(kt, hl=hl, j=j):
                    q0 = j * 512
                    return [(ks_lo[:, kt * 128:(kt + 1) * 128], nq_lo[:, hl, q0:q0 + 512]),
                            (ks_hi[0:64, kt * 128:(kt + 1) * 128], nq_hi[0:64, hl, q0:q0 + 512]),
                            (kaug[0:37, kt * 128:(kt + 1) * 128], qaug[0:37, hl, q0:q0 + 512])]

                def S_win(kt, hl=hl, j=j):
                    q0 = j * 512
                    return [(kw_lo[:, kt * 128:(kt + 1) * 128], nq_lo[:, hl, q0:q0 + 512]),
                            (kw_hi[0:64, kt * 128:(kt + 1) * 128], nq_hi[0:64, hl, q0:q0 + 512]),
                            (kaugw[0:37, kt * 128:(kt + 1) * 128], qaug[0:37, hl, q0:q0 + 512])]

                def mk_out(col, hl=hl, hh=hh, j=j):
                    def out_fn(qb, o, ok, rz, rzk):
                        vtt(rz, rz, gts[:, j * 4 + qb, hh * 3 + col:hh * 3 + col + 1], ALU.mult, [rzk, "gts"], [rzk])
                        vstt(ob[:, qb, hl, :], o, rz, ob[:, qb, hl, :], ALU.mult, ALU.add, [ok, rzk, "ob"], ["ob"])
                    return out_fn

                attn_block(S_sel, own_tiles(j),
                           lambda kt, j=j: (kt - 8 - 4 * j) if kt >= 8 + 4 * j else None,
                           lambda kt: None, lambda kt: vsa[:, kt, :],
                           lambda kt, qb, j=j: kt - 8 > 4 * j + qb,
                           [pb[4], pb[5]], [PB[4], PB[5]], mk_out(1), ["ks", "nq", "kaug", "qaug", "vsa"])
                attn_block(S_win, list(range(4 + 4 * j, 12 + 4 * j)),
                           lambda kt, j=j: 4 + (kt - 4 - 4 * j),
                           lambda kt: None, lambda kt: vwa[:, kt, :],
                           lambda kt, qb, j=j: not (qb <= (kt - 4 - 4 * j) <= qb + 4),
                           [pb[4], pb[5]], [PB[4], PB[5]], mk_out(2), ["kw", "nq", "kaugw", "qaug", "vwa"])
            vcopy(obb[:], ob[:], ["ob"], ["obb"])
            for hl in range(4):
                for qb in range(4):
                    tr(pbt[:, qb * 128:(qb + 1) * 128], obb[:, qb, hl, :], idb[:], ["obb", "idb"], ["pbt"])
                vcopy(o_bT[:, 4 * g + hl, j * 512:(j + 1) * 512], pbt[:, 0:512], ["pbt"], ["o_bT"])
        A.release(mg)
        P.barrier()
    A.release(mC)

    mD = A.mark()
    o_aT = A.alloc([128, 8, TO], BF16)
    wup = A.alloc([128, 16, 256], BF16)
    dma(o_aT[:], oa_scr[:, :, :], "oal", r=["oa_scr"], w=["o_aT"])
    mrg = A.alloc([128, 8, D], BF16)
    tfa = A.alloc([128, 256], F32)
    tfb2 = A.alloc([128, 256], F32)
    for cc in range(8):
        col = cc * 256
        P.dma("pool", lambda h, col=col: h.dma_start(out=wup[:, 0:8, :], in_=w_upf[:, col:col + 256].rearrange("(c p) n -> p c n", p=128)),
              "wupA", writes=["wupA"])
        P.dma("pool", lambda h, col=col: h.dma_start(out=wup[:, 8:16, :], in_=w_upn[:, col:col + 256].rearrange("(c p) n -> p c n", p=128)),
              "wupB", writes=["wupB"])
        wu, wuk = wup, "wupA"
        for which in range(2):
            wt, wk = wload(win_cols((C_GA if which == 0 else C_GB) + col, 256))
            oT = o_aT if which == 0 else o_bT
            for qt in range(8):
                bg, bgk = pb[2 + qt % 2], PB[2 + qt % 2]
                bu, buk = pb[4 + qt % 2], PB[4 + qt % 2]
                proj_tm(bg, bgk, wt, wk, 0, 256, 8 + qt)
                for hh in range(8):
                    mm(bu[:, 0:256], oT[:, hh, qt * 128:(qt + 1) * 128], wu[:, which * 8 + hh, :], hh == 0, hh == 7,
                       ["wupA", "wupB", "o_aT", "o_bT"], [buk])
                act(tfa[:], bg[:, 0:256], AF.Exp, [bgk], ["tfa"], scale=-1.0)
                vts(tfa[:], tfa[:], 1.0, None, ALU.add, None, ["tfa"], ["tfa"])
                vrecip(tfa[:], tfa[:], ["tfa"], ["tfa"])
                if which == 0:
                    vtt(mrg[:, qt, col:col + 256], bu[:, 0:256], tfa[:], ALU.mult, [buk, "tfa"], ["mrg"])
                else:
                    vtt(tfb2[:], bu[:, 0:256], tfa[:], ALU.mult, [buk, "tfa"], ["tfb2"])
                    vtt(mrg[:, qt, col:col + 256], mrg[:, qt, col:col + 256], tfb2[:], ALU.add, ["mrg", "tfb2"], ["mrg"])
    P.barrier()
    mT = xnT
    for qt in range(8):
        for half in range(2):
            for c in range(8):
                ccx = half * 8 + c
                tr(pbt[:, c * 128:(c + 1) * 128], mrg[:, qt, ccx * 128:(ccx + 1) * 128], idb[:], ["mrg", "idb"], ["pbt"])
            vcopy(mT[:, half * 8:half * 8 + 8, qt * 128:(qt + 1) * 128], pbt[:, :].rearrange("p (c n) -> p c n", c=8), ["pbt"], ["mT"])
    P.barrier()
    A.release(mD)
    mE = A.mark()
    hres = A.alloc([128, 8, D], F32)
    xr = [A.alloc([128, 256], F32) for _ in range(2)]
    for cc in range(8):
        col = cc * 256
        wt, wk = wload(w_out[:, col:col + 256].rearrange("(c p) n -> p c n", p=128))
        for qt in range(8):
            b, bk = pb[2 + qt % 2], PB[2 + qt % 2]
            for c in range(16):
                mm(b[:, 0:256], mT[:, c, qt * 128:(qt + 1) * 128], wt[:, c, 0:256], c == 0, c == 15, [wk, "mT"], [bk])
            s = qt % 2
            dma(xr[s][:], x_d[1024 + qt * 128:1024 + (qt + 1) * 128, col:col + 256], "xr%d" % s, w=["xr%d" % s])
            vtt(hres[:, qt, col:col + 256], b[:, 0:256], xr[s][:], ALU.add, [bk, "xr%d" % s], ["hres"])
    gbc2 = A.alloc([128, D], F32)
    xn2 = [A.alloc([128, D], BF16) for _ in range(2)]
    junk2 = A.alloc([128, D], BF16)
    st8 = A.alloc([128, 16], F32)
    dma(gbc2[:], norm_ffn[0:1, :].partition_broadcast(128), "c8", w=["gbc2"])
    hnT = xnT
    for qt in range(8):
        s = qt % 2
        dma(h_scr[qt * 128:(qt + 1) * 128, :], hres[:, qt, :], "hs", r=["hres"], w=["h_scr"])
        act(junk2[:], hres[:, qt, :], AF.Square, ["hres"], ["junk2", "ssq%d" % s], accum=st8[:, s:s + 1])
        act(st8[:, 2 + s:3 + s], st8[:, s:s + 1], AF.Sqrt, ["ssq%d" % s], ["rs%d" % s], bias=epsc[:], scale=1.0 / D)
        vrecip(st8[:, 2 + s:3 + s], st8[:, 2 + s:3 + s], ["rs%d" % s], ["rs%d" % s])
        vstt(xn2[s][:], hres[:, qt, :], st8[:, 2 + s:3 + s], gbc2[:], ALU.mult, ALU.mult, ["hres", "rs%d" % s, "gbc2"], ["xn2%d" % s])
        for half in range(2):
            for c in range(8):
                ccx = half * 8 + c
                tr(pbt[:, c * 128:(c + 1) * 128], xn2[s][:, ccx * 128:(ccx + 1) * 128], idb[:], ["xn2%d" % s, "idb"], ["pbt"])
            vcopy(hnT[:, half * 8:half * 8 + 8, 1024 + qt * 128:1024 + (qt + 1) * 128],
                  pbt[:, :].rearrange("p (c n) -> p c n", c=8), ["pbt"], ["hnT"])
    P.barrier()
    A.release(mE)

    actT = A.alloc([128, 44, TO], BF16)
    tg = [A.alloc([128, 512], F32) for _ in range(2)]
    for fc in range(22):
        f0 = fc * 256
        wg, wgk = wload(w_g[:, f0:f0 + 256].rearrange("(c p) n -> p c n", p=128))
        wu_, wuk_ = wload(w_u[:, f0:f0 + 256].rearrange("(c p) n -> p c n", p=128))
        for sub in range(2):
            for tc in range(2):
                bg, bgk = pb[tc], PB[tc]
                bu, buk = pb[2 + tc], PB[2 + tc]
                for c in range(16):
                    mm(bg[:, :], wg[:, c, sub * 128:(sub + 1) * 128], hnT[:, c, 1024 + tc * 512:1024 + (tc + 1) * 512], c == 0, c == 15, [wgk, "hnT"], [bgk])
                for c in range(16):
                    mm(bu[:, :], wu_[:, c, sub * 128:(sub + 1) * 128], hnT[:, c, 1024 + tc * 512:1024 + (tc + 1) * 512], c == 0, c == 15, [wuk_, "hnT"], [buk])
                t_, tk = tg[tc], "tg%d" % tc
                act(t_[:], bg[:, :], AF.Exp, [bgk], [tk], scale=-1.0)
                vts(t_[:], t_[:], 1.0, None, ALU.add, None, [tk], [tk])
                vrecip(t_[:], t_[:], [tk], [tk])
                vtt(t_[:], t_[:], bg[:, :], ALU.mult, [tk, bgk], [tk])
                vtt(actT[:, fc * 2 + sub, tc * 512:(tc + 1) * 512], t_[:], bu[:, :], ALU.mult, [tk, buk], ["actT"])
    P.barrier()
    mAct = A.mark()
    A.release(mTop)
    wd0 = A.alloc([128, 44, 256], BF16)
    wd1 = A.alloc([128, 44, 256], BF16)
    wds = [wd0, wd1]
    yst = [A.alloc([128, 256], F32) for _ in range(2)]
    hr = [A.alloc([128, 256], F32) for _ in range(2)]
    assert A.off <= mE, (A.off, mE)
    for cc in range(8):
        col = cc * 256
        s = cc % 2
        for part in range(4):
            P.dma("pool", lambda h, s=s, col=col, part=part: h.dma_start(
                out=wds[s][:, part * 11:(part + 1) * 11, :],
                in_=w_d[part * 1408:(part + 1) * 1408, col:col + 256].rearrange("(c p) n -> p c n", p=128)),
                "wd%d" % s, writes=["wd%d" % s])
        for qt in range(8):
            b, bk = pb[qt % 4], PB[qt % 4]
            for f in range(44):
                mm(b[:, 0:256], actT[:, f, qt * 128:(qt + 1) * 128], wds[s][:, f, :], f == 0, f == 43, ["wd%d" % s, "actT"], [bk])
            s2 = qt % 2
            dma(hr[s2][:], h_scr[qt * 128:(qt + 1) * 128, col:col + 256], "hr%d" % s2, r=["h_scr"], w=["hr%d" % s2])
            vtt(yst[s2][:], b[:, 0:256], hr[s2][:], ALU.add, [bk, "hr%d" % s2], ["yst%d" % s2])
            dma(y_d[qt * 128:(qt + 1) * 128, col:col + 256], yst[s2][:], "yout%d" % s2, r=["yst%d" % s2])
    P.run(st, ["yout0", "yout1"])
    st.close()
    return nc, A.peak


_CACHE = {}


def kernel(**inputs):
    x = np.asarray(inputs["x"], dtype=np.float32)
    shared = {}
    for k_ in ("norm_attn", "w_in", "fox_f_bias", "fox_q_gain", "fox_k_gain", "nsa_q_gain", "nsa_kc_gain",
               "nsa_ks_gain", "nsa_kw_gain", "cmp_pe_k", "cmp_w1_k", "cmp_w2_k", "cmp_pe_v", "cmp_w1_v", "cmp_w2_v",
               "w_up_fox", "w_up_nsa", "w_out", "norm_ffn", "w_ffn_gate", "w_ffn_up", "w_ffn_down"):
        a = np.asarray(inputs[k_], dtype=np.float32)
        shared[k_] = np.ascontiguousarray(a[0])
        if shared[k_].ndim == 1:
            shared[k_] = shared[k_][None, :]
    if "nc" not in _CACHE:
        _CACHE["nc"] = build()[0]
    nc = _CACHE["nc"]
    in_maps = []
    for c in range(8):
        b, h = c // 2, c % 2
        if h == 1:
            xc = x[b]
        else:
            xc = np.concatenate([x[b, 0:1024], x[b, 0:1024]], axis=0)
        m = dict(shared)
        m["x"] = np.ascontiguousarray(xc)
        for k_, v in _const_tables(h).items():
            m["c_" + k_] = np.ascontiguousarray(v)
        in_maps.append(m)
    res = run_bass_kernel_spmd(nc, in_maps, core_ids=list(range(8)))
    out = np.empty((4, 2048, 2048), np.float32)
    for c in range(8):
        b, h = c // 2, c % 2
        out[b, h * 1024:(h + 1) * 1024] = res.results[c]["y"]
    return out
```
